# Optimizing a Trainium2 kernel written in Bass

```python
import jax, jax.numpy as jnp
from jax import lax
import numpy as np

D_MODEL = 1024
BATCH = 2
SEQ = 8192
DEPTH = 1
DEC_BATCH = 32
DEC_SEQ = 16
PAST_LEN = 2048

CHUNK = 64
N_META = 16
NORM_EPS = 1e-6
RW_HEADS = 16
RW_HEAD_DIM = 64
RW_WIDTH = RW_HEADS * RW_HEAD_DIM
RW_DECAY_RANK = 64
RW_A_RANK = 64
RW_SHIFT_WIDTH = 3 * RW_WIDTH + RW_DECAY_RANK + RW_A_RANK
RW_GN_EPS = 64e-5
GD_HEADS = 8
GD_HEAD_DIM = 128
GD_WIDTH = GD_HEADS * GD_HEAD_DIM
GD_CONV = 4
GD_CONV_WIDTH = 3 * GD_WIDTH
N_BRANCH = 2
OFF_RW_SHIFT = 0
OFF_RW_GATE = OFF_RW_SHIFT + RW_SHIFT_WIDTH
OFF_GD_CONV = OFF_RW_GATE + RW_WIDTH
OFF_GD_BETA = OFF_GD_CONV + GD_CONV_WIDTH
OFF_GD_ALPHA = OFF_GD_BETA + GD_HEADS
OFF_GD_GATE = OFF_GD_ALPHA + GD_HEADS
OFF_MERGE = OFF_GD_GATE + GD_WIDTH
PROJ_WIDTH = OFF_MERGE + N_BRANCH * D_MODEL

kernel_name = 'rwkv7_gdn_gated_merge_stream_step'


def rmsnorm(x, gain, eps=NORM_EPS):
    xf = x.astype(jnp.float32)
    y = xf * lax.rsqrt(jnp.mean(xf * xf, axis=-1, keepdims=True) + eps)
    return (y * gain.astype(jnp.float32)).astype(x.dtype)


def l2norm(x, eps=1e-6):
    return x * lax.rsqrt(jnp.sum(x * x, axis=-1, keepdims=True) + eps)


def _heads(t, n_heads):
    return t.reshape(t.shape[:-1] + (n_heads, -1))


def rwkv7_branch(xs, gate, S0, p):
    B, T, _ = xs.shape
    f32 = jnp.float32
    xs = xs.astype(f32)
    r, k, v, wl, al = jnp.split(xs, [RW_WIDTH, 2 * RW_WIDTH, 3 * RW_WIDTH, 3 * RW_WIDTH + RW_DECAY_RANK], axis=-1)
    w_log = -jax.nn.softplus(-(p['rw_w0'] + jnp.tanh(wl) @ p['rw_w2'])) - 0.5
    decay = jnp.exp(-jnp.exp(w_log))
    a = jax.nn.sigmoid(p['rw_a0'] + al @ p['rw_a2'])
    kk = l2norm(_heads(k * p['rw_k_k'], RW_HEADS))
    k = k * (1.0 + (a - 1.0) * p['rw_k_a'])
    r, k, v, decay, a = (_heads(t, RW_HEADS) for t in (r, k, v, decay, a))

    def step(S, inp):
        r_t, w_t, k_t, v_t, a_t, b_t = inp
        sa = jnp.einsum('bhvk,bhk->bhv', S, a_t)
        S = S * w_t[:, :, None, :] + sa[..., None] * b_t[:, :, None, :] + v_t[..., None] * k_t[:, :, None, :]
        return S, jnp.einsum('bhvk,bhk->bhv', S, r_t)

    time_major = lambda t: jnp.moveaxis(t, 1, 0)
    S, y = lax.scan(step, S0.astype(f32), tuple(time_major(t) for t in (r, decay, k, v, -kk, kk * a)))
    y = jnp.moveaxis(y, 0, 1)
    mean = jnp.mean(y, axis=-1, keepdims=True)
    var = jnp.mean(jnp.square(y - mean), axis=-1, keepdims=True)
    y = ((y - mean) * lax.rsqrt(var + RW_GN_EPS)).reshape(B, T, RW_WIDTH) * p['rw_ln_w'] + p['rw_ln_b']
    bonus = jnp.sum(r * k * p['rw_r_k'], axis=-1, keepdims=True) * v
    y = (y + bonus.reshape(B, T, RW_WIDTH)) * jax.nn.silu(gate.astype(f32))
    return y, S


def gdn_chunk(S, inp):
    q, k, v, g, beta = inp
    L = q.shape[1]
    q, k, v = (jnp.swapaxes(t, 1, 2) for t in (q, k, v))
    g, beta = jnp.swapaxes(g, 1, 2), jnp.swapaxes(beta, 1, 2)
    G = jnp.cumsum(g, axis=-1)
    idx = jnp.arange(L)
    incl = idx[:, None] >= idx[None, :]
    strict = idx[:, None] > idx[None, :]
    decay = jnp.exp(jnp.where(incl, G[..., :, None] - G[..., None, :], -jnp.inf))
    kk = jnp.einsum('bhik,bhjk->bhij', k, k)
    A = jnp.eye(L, dtype=jnp.float32) + jnp.where(strict, beta[..., :, None] * decay * kk, 0.0)
    rhs = jnp.concatenate([(beta * jnp.exp(G))[..., None] * k, beta[..., None] * v], axis=-1)
    sol = lax.linalg.triangular_solve(A, rhs, left_side=True, lower=True, unit_diagonal=True)
    W, U = sol[..., :GD_HEAD_DIM], sol[..., GD_HEAD_DIM:]
    delta = U - jnp.einsum('bhik,bhkv->bhiv', W, S)
    qk = jnp.einsum('bhik,bhjk->bhij', q, k) * decay
    o = jnp.exp(G)[..., None] * jnp.einsum('bhik,bhkv->bhiv', q, S) + jnp.einsum('bhij,bhjv->bhiv', qk, delta)
    G_last = G[..., -1:]
    S_new = jnp.exp(G_last)[..., None] * S + jnp.einsum('bhjk,bhjv->bhkv', k * jnp.exp(G_last - G)[..., None], delta)
    return S_new, jnp.swapaxes(o, 1, 2)


def gdn_sequence(S, q, k, v, g, beta, n_lead):
    B = q.shape[0]
    outs = []
    if n_lead:
        S, o = gdn_chunk(S, tuple(t[:, :n_lead] for t in (q, k, v, g, beta)))
        outs.append(o)
    rest = [t[:, n_lead:] for t in (q, k, v, g, beta)]
    T = rest[0].shape[1]
    L = min(CHUNK, T)
    n = T // L
    to_blocks = lambda t: jnp.moveaxis(t.reshape((B, n, L) + t.shape[2:]), 1, 0)
    S, o = lax.scan(gdn_chunk, S, tuple(to_blocks(t) for t in rest))
    outs.append(jnp.moveaxis(o, 0, 1).reshape((B, T) + o.shape[3:]))
    return jnp.concatenate(outs, axis=1), S


def gdn_branch(conv_in, conv_prev, beta_logit, alpha_logit, gate, S0, n_lead, p):
    B, T, _ = conv_in.shape
    f32 = jnp.float32
    xp = jnp.concatenate([conv_prev.astype(conv_in.dtype), conv_in], axis=1)
    w = p['gd_conv_w']
    conv = sum(w[i] * xp[:, i:i + T] for i in range(GD_CONV))
    conv_new = xp[:, T:]
    act = jax.nn.silu(conv.astype(f32))
    q, k, v = (_heads(t, GD_HEADS) for t in jnp.split(act, 3, axis=-1))
    q = l2norm(q) * GD_HEAD_DIM ** -0.5
    k = l2norm(k)
    beta = jax.nn.sigmoid(beta_logit.astype(f32))
    g = -jnp.exp(p['gd_a_log'].astype(f32)) * jax.nn.softplus(alpha_logit.astype(f32) + p['gd_dt_bias'])
    o, S = gdn_sequence(S0.astype(f32), q, k, v, g, beta, n_lead)
    o = rmsnorm(o, p['gd_norm_w']).reshape(B, T, GD_WIDTH) * jax.nn.silu(gate.astype(f32))
    return o, conv_new, S


def trunk_layer(h, n_lead, shift_prev, conv_prev, s_rw, s_gd, p):
    B, T, _ = h.shape
    xn = rmsnorm(h, p['norm_pre'])
    proj = xn @ p['w_in']
    ps = proj[..., OFF_RW_SHIFT:OFF_RW_GATE]
    prev = jnp.concatenate([shift_prev[:, None, :].astype(ps.dtype), ps[:, :-1]], axis=1)
    xs = ps + p['rw_mu'] * (prev - ps)
    y_rw, s_rw_new = rwkv7_branch(xs, proj[..., OFF_RW_GATE:OFF_GD_CONV], s_rw, p)
    y_gd, conv_new, s_gd_new = gdn_branch(proj[..., OFF_GD_CONV:OFF_GD_BETA], conv_prev,
                                          proj[..., OFF_GD_BETA:OFF_GD_ALPHA], proj[..., OFF_GD_ALPHA:OFF_GD_GATE],
                                          proj[..., OFF_GD_GATE:OFF_MERGE], s_gd, n_lead, p)
    gates = jax.nn.sigmoid(proj[..., OFF_MERGE:].astype(jnp.float32)).reshape(B, T, N_BRANCH, D_MODEL)
    merged = gates[..., 0, :] * (y_rw @ p['w_out_a']) + gates[..., 1, :] * (y_gd @ p['w_out_b'])
    out = (merged @ p['w_out']).astype(h.dtype)
    h = h + rmsnorm(out, p['norm_post'])
    dt = h.dtype
    return h, (ps[:, -1].astype(dt), s_rw_new.astype(dt), conv_new.astype(dt), s_gd_new.astype(dt))


def setup_inputs(seed: int = 0) -> dict:
    key = jax.random.key(seed)
    ks = iter(jax.random.split(key, 32))
    nrm = lambda shape, s: jax.random.normal(next(ks), shape, jnp.float32) * s
    unif = lambda shape, lo, hi: jax.random.uniform(next(ks), shape, jnp.float32, lo, hi)
    Ly = (DEPTH,)
    return {
        'x_prompt': nrm((BATCH, SEQ, D_MODEL), 1.0),
        'x_sample': nrm((DEC_BATCH, DEC_SEQ, D_MODEL), 1.0),
        'state_rwkv_shift': nrm(Ly + (DEC_BATCH, RW_SHIFT_WIDTH), 1.0),
        'state_rwkv_wkv': nrm(Ly + (DEC_BATCH, RW_HEADS, RW_HEAD_DIM, RW_HEAD_DIM), 0.3),
        'state_gdn_conv': nrm(Ly + (DEC_BATCH, GD_CONV - 1, GD_CONV_WIDTH), 1.0),
        'state_gdn_ssm': nrm(Ly + (DEC_BATCH, GD_HEADS, GD_HEAD_DIM, GD_HEAD_DIM), 0.1),
        'meta_tokens': nrm((N_META, D_MODEL), 1.0),
        'norm_pre': 1.0 + nrm(Ly + (D_MODEL,), 0.05),
        'w_in': nrm(Ly + (D_MODEL, PROJ_WIDTH), D_MODEL ** -0.5),
        'rw_mu': unif(Ly + (RW_SHIFT_WIDTH,), 0.0, 1.0),
        'rw_w0': nrm(Ly + (RW_WIDTH,), 0.5),
        'rw_w2': nrm(Ly + (RW_DECAY_RANK, RW_WIDTH), RW_DECAY_RANK ** -0.5),
        'rw_a0': nrm(Ly + (RW_WIDTH,), 0.5),
        'rw_a2': nrm(Ly + (RW_A_RANK, RW_WIDTH), RW_A_RANK ** -0.5),
        'rw_k_k': 0.85 + nrm(Ly + (RW_WIDTH,), 0.05),
        'rw_k_a': 1.0 + nrm(Ly + (RW_WIDTH,), 0.05),
        'rw_r_k': nrm(Ly + (RW_HEADS, RW_HEAD_DIM), 0.1),
        'rw_ln_w': 1.0 + nrm(Ly + (RW_WIDTH,), 0.05),
        'rw_ln_b': nrm(Ly + (RW_WIDTH,), 0.02),
        'gd_conv_w': nrm(Ly + (GD_CONV, GD_CONV_WIDTH), GD_CONV ** -0.5),
        'gd_a_log': jnp.log(unif(Ly + (GD_HEADS,), 1.0, 8.0)),
        'gd_dt_bias': nrm(Ly + (GD_HEADS,), 0.5) - 2.0,
        'gd_norm_w': 1.0 + nrm(Ly + (GD_HEAD_DIM,), 0.05),
        'w_out_a': nrm(Ly + (RW_WIDTH, D_MODEL), RW_WIDTH ** -0.5),
        'w_out_b': nrm(Ly + (GD_WIDTH, D_MODEL), GD_WIDTH ** -0.5),
        'w_out': nrm(Ly + (D_MODEL, D_MODEL), D_MODEL ** -0.5),
        'norm_post': 1.0 + nrm(Ly + (D_MODEL,), 0.05),
    }


def reference(x_prompt, x_sample, state_rwkv_shift, state_rwkv_wkv, state_gdn_conv, state_gdn_ssm,
              meta_tokens, norm_pre, w_in, rw_mu, rw_w0, rw_w2, rw_a0, rw_a2, rw_k_k, rw_k_a, rw_r_k,
              rw_ln_w, rw_ln_b, gd_conv_w, gd_a_log, gd_dt_bias, gd_norm_w, w_out_a, w_out_b, w_out, norm_post):
    Bp = x_prompt.shape[0]
    dtp = x_prompt.dtype
    meta = jnp.broadcast_to(meta_tokens.astype(dtp)[None], (Bp, N_META, D_MODEL))
    hp = jnp.concatenate([meta, x_prompt], axis=1)
    hs = x_sample
    p_states = [[], [], [], []]
    s_states = [[], [], [], []]
    for l in range(DEPTH):
        p = {
            'norm_pre': norm_pre[l], 'w_in': w_in[l], 'rw_mu': rw_mu[l], 'rw_w0': rw_w0[l], 'rw_w2': rw_w2[l],
            'rw_a0': rw_a0[l], 'rw_a2': rw_a2[l], 'rw_k_k': rw_k_k[l], 'rw_k_a': rw_k_a[l], 'rw_r_k': rw_r_k[l],
            'rw_ln_w': rw_ln_w[l], 'rw_ln_b': rw_ln_b[l], 'gd_conv_w': gd_conv_w[l], 'gd_a_log': gd_a_log[l],
            'gd_dt_bias': gd_dt_bias[l], 'gd_norm_w': gd_norm_w[l], 'w_out_a': w_out_a[l], 'w_out_b': w_out_b[l],
            'w_out': w_out[l], 'norm_post': norm_post[l],
        }
        hp, new_p = trunk_layer(
            hp, N_META,
            jnp.zeros((Bp, RW_SHIFT_WIDTH), dtp),
            jnp.zeros((Bp, GD_CONV - 1, GD_CONV_WIDTH), dtp),
            jnp.zeros((Bp, RW_HEADS, RW_HEAD_DIM, RW_HEAD_DIM), jnp.float32),
            jnp.zeros((Bp, GD_HEADS, GD_HEAD_DIM, GD_HEAD_DIM), jnp.float32),
            p)
        hs, new_s = trunk_layer(hs, 0, state_rwkv_shift[l], state_gdn_conv[l], state_rwkv_wkv[l], state_gdn_ssm[l], p)
        for i in range(4):
            p_states[i].append(new_p[i])
            s_states[i].append(new_s[i])
    p_shift, p_wkv, p_conv, p_ssm = (jnp.stack(t, axis=0) for t in p_states)
    s_shift, s_wkv, s_conv, s_ssm = (jnp.stack(t, axis=0) for t in s_states)
    y_prompt = hp[:, N_META:]
    y_sample = hs
    return (y_prompt, y_sample, p_shift, p_wkv, p_conv, p_ssm, s_shift, s_wkv, s_conv, s_ssm)
```

```python
import numpy as np
from contextlib import ExitStack
import concourse.bass as bass
import concourse.mybir as mybir
from concourse.bass_utils import run_bass_kernel_spmd

F32 = mybir.dt.float32
BF16 = mybir.dt.bfloat16
I32 = mybir.dt.int32
AF = mybir.ActivationFunctionType
ALU = mybir.AluOpType
AX = mybir.AxisListType


class View:
    __slots__ = ("buf", "ap")

    def __init__(self, buf, ap):
        self.buf = buf
        self.ap = ap

    def __getitem__(self, idx):
        return View(self.buf, self.ap[idx])

    def re(self, pat, **kw):
        return View(self.buf, self.ap.rearrange(pat, **kw))

    def bc(self, shape):
        return View(self.buf, self.ap.to_broadcast(shape))


class Sub:
    def __init__(self, buf, off):
        self.buf = buf
        self.off = off

    def __getitem__(self, idx):
        r, c = idx
        a = (c.start or 0) + self.off
        b = (c.stop if c.stop is not None else 128) + self.off
        return self.buf[r, a:b]


class Buf:
    def __init__(self, name, t, space):
        self.name = name
        self.t = t
        self.space = space
        self.lw = None
        self.rd = {}
        self.sem = None
        self.dma_val = 0

    def __getitem__(self, idx):
        return View(self, self.t[idx])

    def full(self):
        return View(self, self.t.ap() if hasattr(self.t, "ap") and callable(getattr(self.t, "ap")) else self.t[:])


class K:
    def __init__(self, nc, es):
        self.nc = nc
        self.es = es
        self.eng = {"pe": nc.tensor, "act": nc.scalar, "dve": nc.vector, "pool": nc.gpsimd, "sp": nc.sync}
        self.esem = {}
        self.cnt = {}
        self.known = {}
        for e in self.eng:
            self.esem[e] = es.enter_context(nc.semaphore("es_" + e))
            self.cnt[e] = 0
            self.known[e] = {}
        self.nbuf = 0
        self.ninst = 0

    def sb(self, name, shape, dt=F32):
        t = self.es.enter_context(self.nc.sbuf_tensor("s_" + name, list(shape), dt))
        return Buf(name, t, "sb")

    def ps(self, name, shape, dt=F32):
        t = self.es.enter_context(self.nc.psum_tensor("p_" + name, list(shape), dt))
        return Buf(name, t, "ps")

    def dram(self, name, shape, dt, kind):
        t = self.nc.dram_tensor(name, list(shape), dt, kind=kind)
        return Buf(name, t, "dram")

    def _wait(self, e, tok):
        if tok is None:
            return
        if tok[0] == "dma":
            b = tok[1]
            key, sem, val = ("d", id(b)), b.sem, b.dma_val
        else:
            if tok[0] == e and e == "pe":
                return
            key, sem, val = ("e", tok[0]), self.esem[tok[0]], tok[1]
        kn = self.known[e]
        if kn.get(key, 0) >= val:
            return
        self.eng[e].wait_ge(sem, val)
        kn[key] = val

    def _deps(self, e, reads, writes):
        for v in reads:
            self._wait(e, v.buf.lw)
        for v in writes:
            b = v.buf
            self._wait(e, b.lw)
            for tok in list(b.rd.values()):
                self._wait(e, tok)

    def _mark(self, tok, reads, writes):
        for v in reads:
            b = v.buf
            key = ("d", id(tok[1])) if tok[0] == "dma" else tok[0]
            b.rd[key] = tok
        for v in writes:
            b = v.buf
            b.lw = tok
            b.rd = {}

    def op(self, e, fn, reads, writes):
        self._deps(e, reads, writes)
        ins = fn()
        self.cnt[e] += 1
        ins.then_inc(self.esem[e], 1)
        self._mark((e, self.cnt[e]), reads, writes)
        self.ninst += 1
        return ins

    def dma(self, e, out, in_, **kw):
        self._deps(e, [in_], [out])
        sbv = out if out.buf.space != "dram" else in_
        b = sbv.buf
        if b.sem is None:
            b.sem = self.es.enter_context(self.nc.semaphore("ds_%s" % b.name))
        ins = self.eng[e].dma_start(out=out.ap, in_=in_.ap, **kw)
        b.dma_val += 16
        ins.then_inc(b.sem, 16)
        self._mark(("dma", b), [in_], [out])
        self.ninst += 1
        return ins

    def drain(self, e, bufs):
        for b in bufs:
            self._wait(e, b.lw)
            for tok in list(b.rd.values()):
                self._wait(e, tok)

    def mm(self, out, lhsT, rhs, start=True, stop=True, **kw):
        return self.op("pe", lambda: self.nc.tensor.matmul(out.ap, lhsT.ap, rhs.ap, start=start, stop=stop, **kw),
                       [lhsT, rhs] + ([] if start else [out]), [out])

    def tr(self, out, in_, ident):
        return self.op("pe", lambda: self.nc.tensor.transpose(out.ap, in_.ap, ident.ap), [in_, ident], [out])

    def act(self, out, in_, func, bias=None, scale=None, accum=None, e="act"):
        kw = {}
        rd = [in_]
        wr = [out]
        if bias is not None:
            if isinstance(bias, View):
                kw["bias"] = bias.ap
                rd.append(bias)
            else:
                kw["bias"] = bias
        if scale is not None:
            if isinstance(scale, View):
                kw["scale"] = scale.ap
                rd.append(scale)
            else:
                kw["scale"] = scale
        if accum is not None:
            kw["accum_out"] = accum.ap
            wr.append(accum)
        return self.op("act", lambda: self.nc.scalar.activation(out.ap, in_.ap, func, **kw), rd, wr)

    def _veng(self, e):
        return self.nc.vector if e == "dve" else self.nc.gpsimd

    def tt(self, out, a, b, op, e="dve"):
        return self.op(e, lambda: self._veng(e).tensor_tensor(out.ap, a.ap, b.ap, op), [a, b], [out])

    def ts(self, out, a, s1, op0, s2=None, op1=None, e="dve", accum=None):
        rd = [a]
        wr = [out]
        a1 = s1.ap if isinstance(s1, View) else s1
        a2 = s2.ap if isinstance(s2, View) else s2
        if isinstance(s1, View):
            rd.append(s1)
        if isinstance(s2, View):
            rd.append(s2)
        kw = {}
        if op1 is not None:
            kw["op1"] = op1
        if accum is not None:
            kw["accum_out"] = accum.ap
            wr.append(accum)
        return self.op(e, lambda: self._veng(e).tensor_scalar(out.ap, a.ap, a1, a2, op0, **kw), rd, wr)

    def stt(self, out, a, s, b, op0, op1, e="dve"):
        rd = [a, b]
        a1 = s.ap if isinstance(s, View) else s
        if isinstance(s, View):
            rd.append(s)
        return self.op(e, lambda: self._veng(e).scalar_tensor_tensor(out.ap, a.ap, a1, b.ap, op0, op1), rd, [out])

    def cp(self, out, a, e="dve"):
        if e == "act":
            return self.op("act", lambda: self.nc.scalar.copy(out.ap, a.ap), [a], [out])
        return self.op(e, lambda: self._veng(e).tensor_copy(out.ap, a.ap), [a], [out])

    def memset(self, out, val, e="dve"):
        return self.op(e, lambda: self._veng(e).memset(out.ap, val), [], [out])

    def scan(self, out, d0, d1, init, op0, op1):
        return self.op("dve", lambda: self.nc.vector.tensor_tensor_scan(out.ap, d0.ap, d1.ap, init, op0, op1), [d0, d1], [out])


D = 1024
NG = 7
WCOL = 7 * 128 + 258
NEGE = -float(np.exp(-0.5))


_DBG = {}


class _Cut(Exception):
    pass


def ck(n):
    if _DBG.get('cut') == n:
        raise _Cut()


def build(NPC, NSP, debug=False, phases=3):
    NS = 1 + NPC + NSP
    NSLOT = NS * 128
    NT3 = (NS + 7) // 8
    NOUT = 1 + NSP
    nc = bass.Bass("TRN2", target_bir_lowering=False)
    es = ExitStack()
    with es:
        k = K(nc, es)
        di = lambda n, s, dt=F32: k.dram(n, s, dt, "ExternalInput")
        do = lambda n, s, dt=F32: k.dram(n, s, dt, "ExternalOutput")
        XS = di("xs", [NSLOT, D])
        X3 = di("x3", [NT3 * 128, D])
        IDX = di("idx", [128, NT3 * 2], I32)
        WC = di("wc", [D, WCOL])
        WG = di("wg", [D, 2048])
        WOA = di("woa", [D, D]); WOB = di("wob", [D, D]); WOUT = di("wout", [D, D])
        W2A2 = di("w2a2", [128, 128])
        PRMd = di("prm", [128, 32])
        GNd = di("gn", [128, 8])
        BCd = di("bc", [3, 128, 128])
        NPd = di("npost", [128, D])
        CSTd = di("cst", [7, 128, 128])
        SEL2d = di("sel2", [128, 4])
        SH0 = di("sh0", [max(NSP, 1), 4, 128, 2])
        CV0 = di("cv0", [max(NSP, 1), 3, 128, 6])
        HB0 = di("hb0", [max(NSP, 1), 2, 128, 128])
        SS0 = di("ss0", [max(NSP, 1), 2, 128, 128])
        OY = do("oy", [NT3 * 128, D])
        OSH = do("osh", [NOUT, 128, 8])
        OCV = do("ocv", [NOUT, 128, 18])
        OWK = do("owk", [NOUT, 2, 128, 128])
        OSS = do("oss", [NOUT, 2, 128, 128])
        CH = _DBG.get('ch', 80)
        NCH = (NS + CH - 1) // CH
        chn = [min(CH, NS - kc * CH) for kc in range(NCH)]
        YEXs = [k.dram("yex%d" % kc, [chn[kc] * 32, 1024], BF16, "Internal") for kc in range(NCH)]
        YALLs = [k.dram("yall%d" % kc, [8 * chn[kc] * 32, 1024], BF16, "Internal") for kc in range(NCH)]
        YSC = [k.dram("ysc%d" % j, [256, 1024], BF16, "Internal") for j in range(NT3)]
        if debug:
            DBG = do("dbg", [NSLOT, 256], BF16)
        es.enter_context(nc.Block())

        cst = [k.sb("cst%d" % i, [128, 128]) for i in range(7)]
        IDENT, ML, MU, MUI, BLK, SMROW, ONES = cst
        for i in range(7):
            k.dma("sp", cst[i][:, :], CSTd[i])
        NML = k.sb("nml", [128, 128]); NMU = k.sb("nmu", [128, 128])
        IDB = k.sb("idb", [128, 128], BF16)
        SEL2 = k.sb("sel2", [128, 4]); k.dma("sp", SEL2[:, :], SEL2d[:, :])
        SMCOL = SEL2[:, 2:3]
        PRM = k.sb("prm", [128, 32]); k.dma("sp", PRM[:, :], PRMd[:, :])
        GN = k.sb("gn", [128, 8]); k.dma("sp", GN[:, :], GNd[:, :])
        LNW = k.sb("lnw", [128, 128]); LNB = k.sb("lnb", [128, 128]); GNW = k.sb("gnw", [128, 128])
        k.dma("sp", LNW[:, :], BCd[0]); k.dma("sp", LNB[:, :], BCd[1]); k.dma("sp", GNW[:, :], BCd[2])
        NPB = k.sb("npb", [128, D]); k.dma("sp", NPB[:, :], NPd[:, :])
        W2 = k.sb("w2a2", [128, 128]); k.dma("sp", W2[:, :], W2A2[:, :])
        IDXs = k.sb("idxs", [128, NT3 * 2], I32); k.dma("sp", IDXs[:, :], IDX[:, :])
        k.ts(NML[:, :], ML[:, :], -1.0, ALU.mult)
        k.ts(NMU[:, :], MU[:, :], -1.0, ALU.mult)
        k.cp(IDB[:, :], IDENT[:, :])
        P_MU = 0
        P_W0, P_A0, P_KK, P_KA, P_RK = 4, 5, 6, 7, 8
        P_CW = 9
        P_DTB, P_ALOG = 21, 22
        P_EPS, P_EPSGN, P_ONE, P_EPSL2 = 23, 24, 25, 26
        P_OMKA, P_NEGA, P_NW0, P_NA0 = 27, 28, 29, 30
        pc = lambda j: PRM[:, j:j + 1]
        k.ts(pc(P_OMKA), pc(P_KA), -1.0, ALU.mult, 1.0, ALU.add)
        k.act(pc(P_NEGA), pc(P_ALOG), AF.Exp)
        k.ts(pc(P_NEGA), pc(P_NEGA), -1.0, ALU.mult)
        k.ts(pc(P_NW0), pc(P_W0), -1.0, ALU.mult)
        k.ts(pc(P_NA0), pc(P_A0), -1.0, ALU.mult)

        Wc = k.sb("Wc", [128, 8, WCOL], BF16)
        Wg = k.sb("Wg", [128, 8, 2048], BF16)
        Woa = k.sb("Woa", [128, 8, D], BF16); Wob = k.sb("Wob", [128, 8, D], BF16); Wout = k.sb("Wout", [128, 8, D], BF16)
        STG = [k.sb("stg0", [128, WCOL])] * 2
        nst = 0
        for kt in range(8):
            st = STG[nst % 2]; nst += 1
            k.dma("sp", st[:, 0:WCOL], WC[kt * 128:(kt + 1) * 128, :])
            k.ts(Wc[:, kt, :], st[:, 0:WCOL], GN[:, kt:kt + 1], ALU.mult, e="pool")

        XB = [k.sb("xb%d" % i, [128, D]) for i in range(2)]
        XN = k.sb("xn", [128, D])
        SQJ = XN
        XNT = k.sb("xnt", [128, 8, 128], BF16)
        col = lambda n, w=1: k.sb(n, [128, w])
        SS = col("ss"); RSTD = col("rstd")
        PG = [k.sb("pg%d" % g, [128, 2, 67]) for g in range(NG)]
        sq = lambda n, dt=F32: k.sb(n, [128, 128], dt)
        XSr, XSk, XSv, XSwa, DTMP = sq("xsr"), sq("xsk"), sq("xsv"), sq("xswa"), sq("dtmp")
        TW = sq("tw"); SW = sq("sw"); AGT = sq("agt"); LW = sq("lw"); KKR = sq("kkr"); SQK = sq("sqk"); RN = sq("rn")
        KK = sq("kk"); T1 = sq("t1"); KM = sq("km"); Bt = sq("bt_"); LWT = sq("lwt"); LAM = sq("lam"); LAMX = sq("lamx")
        LL = col("ll", 2); GLc = col("glc", 2)
        EP, EM, EX, EL = sq("ep"), sq("em"), sq("ex"), sq("el")
        RT, AT, BT, KT, BH, KH = sq("rt"), sq("at"), sq("btt"), sq("kt"), sq("bh"), sq("kh")
        RTP = k.sb("rtp", [128, 192]); ATP = k.sb("atp", [128, 192]); WTP = k.sb("wtp", [128, 192]); QGP = k.sb("qgp", [128, 192])
        for b in (RTP, ATP, WTP, QGP):
            k.memset(b[:, :], 0.0, e="pool")
        VBK = k.sb("vbk", [128, 384])
        Vtm, BHtm, KHtm = Sub(VBK, 0), Sub(VBK, 128), Sub(VBK, 256)
        SC = [[sq("sc%d_%d" % (h, i)) for i in range(5)] for h in range(2)]
        NB = [[sq("nb%d_%d" % (j, i)) for i in range(6)] for j in range(3)]
        RUs, Us, Ys, YC, YN, RKR = sq("rus"), sq("us"), sq("ys"), sq("yc"), sq("yn"), sq("rkr")
        SBs = col("sbs", 2); SUMc = col("sumc", 2); MEAN = col("mean", 2); VSc = col("vsc", 2); RSG = col("rsg", 2)
        Hbd = [sq("hbd%d" % q) for q in range(2)]
        Sg = [sq("sg%d" % q) for q in range(2)]
        SGR, SGG = sq("sgr"), sq("sgg")
        STMP = sq("stmp")
        BA = col("ba", 2)
        CV = [sq("cv%d" % g) for g in range(3)]
        AC = [sq("ac%d" % g) for g in range(3)]
        SSQ, SSK, RNQ, RNK = col("ssq"), col("ssk"), col("rnq"), col("rnk")
        BETA, E1, SPc, Gc = col("beta"), col("e1"), col("spc"), col("gc")
        GCs = col("gcs", 2); EG = col("eg"); DG = col("dg"); EGL = col("egl"); EGB = col("egb", 2)
        GROW = k.sb("grow", [1, 128]); NGROW = k.sb("ngrow", [1, 128])
        DMc, DEC, DMTc, DECT = sq("dmc"), sq("dec"), sq("dmtc"), sq("dect")
        KN, QN, QG, KB, KW, VB, KL, GBC = sq("kn"), sq("qn"), sq("qg"), sq("kb"), sq("kw"), sq("vb"), sq("kl"), sq("gbc")
        KKQ = k.sb("kkq", [128, 384])
        KNT, KBT, QNT = Sub(KKQ, 0), Sub(KKQ, 128), Sub(KKQ, 256)
        PTs, DEL, Os, TMPg = sq("pts"), sq("del"), sq("os"), sq("tmpg")
        SSO, RO = col("sso"), col("ro")
        YOUT = k.sb("yout", [128, 256], BF16)
        OSHs = k.sb("oshs", [128, 8]); OCVs = k.sb("ocvs", [128, 18])

        PW = [k.ps("pw%d" % i, [128, 512]) for i in range(2)]
        pnb = [es.enter_context(nc.psum_tensor("pnb%d" % i, [128, 512], F32)) for i in range(5)]
        PN = [Buf("pn%d" % i, pnb[i][:, 0:128], "ps") for i in range(5)]
        pbb = es.enter_context(nc.psum_tensor("pbb", [128, 1024], BF16))
        PB = [Buf("pb0", pbb[:, 0:512], "ps")] * 2
        st_ = {"pn": 0, "pw": 0, "ev": 0}

        def pn():
            st_["pn"] += 1
            return PN[st_["pn"] % 5]

        def pw():
            st_["pw"] += 1
            return PW[st_["pw"] % 2]

        def evac(out, in_):
            st_["ev"] += 1
            if st_["ev"] % 2 or _DBG.get('evact'):
                k.cp(out, in_, e="act")
            else:
                k.cp(out, in_, e="dve")

        v3 = lambda b: b[:, :].re("p (s c) -> p s c", s=2)
        padv = lambda b: b[:, :].re("p (s c) -> p s c", s=3)[:, 0:3:2, :]

        def rsqrt(out, in_, scale, epscol):
            k.act(out, in_, AF.Ln, bias=epscol, scale=scale)
            k.act(out, out, AF.Exp, scale=-0.5)

        def recip1p(t):
            k.ts(t, t, 1.0, ALU.add, e="pool")
            k.op("dve", lambda: nc.vector.reciprocal(t.ap, t.ap), [t], [t])

        def sigm(out, in_, nbias=None):
            if nbias is None:
                k.act(out, in_, AF.Exp, scale=-1.0)
            else:
                k.act(out, in_, AF.Exp, bias=nbias, scale=-1.0)
            recip1p(out)

        def silu(out, in_, tmp):
            k.act(tmp, in_, AF.Exp, scale=-1.0)
            recip1p(tmp)
            k.tt(out, in_, tmp, ALU.mult)

        def tanh_(out, in_):
            k.act(out, in_, AF.Exp, scale=-2.0)
            recip1p(out)
            k.ts(out, out, 2.0, ALU.mult, -1.0, ALU.add, e="pool")

        def norm_transpose(X):
            k.act(SQJ[:, :], X[:, :], AF.Square, accum=SS[:, :])
            rsqrt(RSTD[:, :], SS[:, :], 1.0 / D, pc(P_EPS))
            ck(98)
            k.ts(XN[:, :], X[:, :], RSTD[:, :], ALU.mult)
            ck(99)
            for half in range(2):
                p = pw()
                for j in range(4):
                    kt = half * 4 + j
                    k.tr(p[:, j * 128:(j + 1) * 128], XN[:, kt * 128:(kt + 1) * 128], IDENT[:, :])
                evac(XNT[:, half * 4:half * 4 + 4, :], p[:, :].re("p (a b) -> p a b", a=4))

        def neumann(j, P0, PT0, nlev=5):
            nb = NB[j]
            XT = nb[0]
            k.tt(XT[:, :], IDENT[:, :], PT0[:, :], ALU.add)
            P, PT = P0, PT0
            alt = [(nb[1], nb[2]), (nb[3], nb[4])]
            xts = [nb[5], nb[0]]
            for lvl in range(1, nlev + 1):
                Pn, PTn = alt[lvl % 2]
                p1 = pn()
                k.mm(p1[:, :], PT[:, :], P[:, :])
                evac(Pn[:, :], p1[:, :])
                if lvl < nlev:
                    p2 = pn()
                    k.mm(p2[:, :], P[:, :], PT[:, :])
                    evac(PTn[:, :], p2[:, :])
                p3 = pn()
                k.mm(p3[:, :], Pn[:, :], XT[:, :])
                XTn = xts[(lvl - 1) % 2]
                k.tt(XTn[:, :], XT[:, :], p3[:, :], ALU.add)
                XT = XTn
                P, PT = Pn, PTn
            return XT

        out_i = [0]

        def step(s):
            kind = "lead" if s == 0 else ("prompt" if s <= NPC else "sample")
            masked = kind != "prompt"
            last = 15 if masked else 63
            X = XB[s % 2]
            k.dma("sp", X[:, :], XS[s * 128:(s + 1) * 128, :])
            if kind == "lead":
                for g in range(NG):
                    k.memset(PG[g][:, :, 0:3], 0.0, e="pool")
                for q in range(2):
                    k.memset(Hbd[q][:, :], 0.0, e="pool")
                    k.memset(Sg[q][:, :], 0.0, e="pool")
            if kind == "sample":
                sp = s - 1 - NPC
                for g in range(4):
                    k.dma("sp", PG[g][:, :, 2:3], View(SH0, SH0.t[sp, g].rearrange("p (q o) -> p q o", o=1)), allow_slow_non_contiguous=True)
                for g in range(3):
                    k.dma("sp", PG[4 + g][:, :, 0:3], View(CV0, CV0.t[sp, g].rearrange("p (q h) -> p q h", q=2)))
                for q in range(2):
                    k.dma("sp", Hbd[q][:, :], HB0[sp, q])
                    k.dma("sp", Sg[q][:, :], SS0[sp, q])
            ck(1)
            norm_transpose(X)
            ck(100)
            for g in range(_DBG.get('ng', NG)):
                p = pn()
                for kt in range(8):
                    k.mm(p[:, :], Wc[:, kt, g * 128:(g + 1) * 128], XNT[:, kt, :], start=(kt == 0), stop=(kt == 7))
                ck(101)
                evac(PG[g][:, :, 3:67], p[:, :].re("p (s c) -> p s c", s=2))
                ck(102)
            ck(2)
            p = pw()
            for kt in range(8):
                k.mm(p[:, 0:258], XNT[:, kt, :], Wc[:, kt, 896:1154], start=(kt == 0), stop=(kt == 7))
            silu(SGR[:, :], p[:, 0:128], STMP[:, :])
            silu(SGG[:, :], p[:, 128:256], STMP[:, :])
            k.cp(BA[:, :], p[:, 256:258])

            ck(3)
            for g, xs in enumerate((XSr, XSk, XSv, XSwa)):
                k.tt(v3(DTMP), PG[g][:, :, 2:66], PG[g][:, :, 3:67], ALU.subtract, e="pool")
                k.stt(v3(xs), v3(DTMP), pc(P_MU + g), PG[g][:, :, 3:67], ALU.mult, ALU.add)
            ck(4)
            tanh_(TW[0:64, :], XSwa[0:64, :])
            pz = pn(); pa = pn()
            k.mm(pz[:, :], W2[0:64, :], TW[0:64, :])
            k.mm(pa[:, :], W2[64:128, :], XSwa[64:128, :])
            sigm(SW[:, :], pz[:, :], pc(P_NW0))
            sigm(AGT[:, :], pa[:, :], pc(P_NA0))
            k.ts(LW[:, :], SW[:, :], NEGE, ALU.mult, e="pool")
            k.ts(KKR[:, :], XSk[:, :], pc(P_KK), ALU.mult, e="pool")
            k.tt(SQK[:, :], KKR[:, :], KKR[:, :], ALU.mult, e="pool")
            pss = pn()
            k.mm(pss[:, :], BLK[:, :], SQK[:, :])
            rsqrt(RN[:, :], pss[:, :], 1.0, pc(P_EPSL2))
            k.tt(KK[:, :], KKR[:, :], RN[:, :], ALU.mult, e="pool")
            k.ts(T1[:, :], AGT[:, :], pc(P_KA), ALU.mult, pc(P_OMKA), ALU.add, e="pool")
            k.tt(KM[:, :], XSk[:, :], T1[:, :], ALU.mult, e="pool")
            if masked:
                k.tt(LW[:, :], LW[:, :], SMROW[:, :], ALU.mult)
                k.tt(KK[:, :], KK[:, :], SMROW[:, :], ALU.mult)
                k.tt(KM[:, :], KM[:, :], SMROW[:, :], ALU.mult)
            k.tt(Bt[:, :], KK[:, :], AGT[:, :], ALU.mult, e="pool")
            ck(5)
            pt_ = pn()
            k.tr(pt_[:, :], LW[:, :], IDENT[:, :])
            evac(LWT[:, :], pt_[:, :])
            pl = pn()
            k.mm(pl[:, :], LWT[:, :], MUI[:, :])
            k.cp(LAM[:, :], pl[:, :], e="dve")
            k.tt(LAMX[:, :], LAM[:, :], LW[:, :], ALU.subtract, e="pool")
            k.cp(LL[:, :], v3(LAM)[:, :, 63], e="pool")
            k.act(EP[:, :], LAM[:, :], AF.Exp)
            k.act(EM[:, :], LAM[:, :], AF.Exp, scale=-1.0)
            k.act(EX[:, :], LAMX[:, :], AF.Exp)
            for q in range(2):
                k.act(EL[:, q * 64:(q + 1) * 64], LAM[:, q * 64:(q + 1) * 64], AF.Exp, bias=LL[:, q:q + 1], scale=-1.0)
            k.act(GLc[:, :], LL[:, :], AF.Exp)
            k.tt(RT[:, :], XSr[:, :], EP[:, :], ALU.mult, e="pool")
            k.stt(AT[:, :], KK[:, :], -1.0, EX[:, :], ALU.mult, ALU.mult)
            k.tt(BT[:, :], Bt[:, :], EM[:, :], ALU.mult, e="pool")
            k.tt(KT[:, :], KM[:, :], EM[:, :], ALU.mult, e="pool")
            k.tt(BH[:, :], Bt[:, :], EL[:, :], ALU.mult, e="pool")
            k.tt(KH[:, :], KM[:, :], EL[:, :], ALU.mult, e="pool")
            k.cp(padv(RTP), v3(RT), e="pool")
            k.cp(padv(ATP), v3(AT), e="pool")
            p = pw()
            for i_, src in enumerate((XSv, BH, KH)):
                k.tr(p[:, i_ * 128:(i_ + 1) * 128], src[:, :], IDENT[:, :])
            evac(VBK[:, :], p[:, 0:384])
            ck(6)
            ainv = []
            for h in range(2):
                hs = slice(h * 64, (h + 1) * 64)
                specs = ((AT, BT, ML), (BT, AT, MU), (KT, AT, MU), (BT, RT, MUI), (KT, RT, MUI))
                for i, (l, r, m) in enumerate(specs):
                    p = pn()
                    k.mm(p[:, :], l[hs, :], r[hs, :])
                    k.tt(SC[h][i][:, :], p[:, :], m[:, :], ALU.mult)
                ainv.append(neumann(h, SC[h][0], SC[h][1], 3 if masked else 5))
            ck(7)
            pr = pn()
            k.mm(pr[:, :], ATP[:, 0:128], Hbd[0][:, :], start=True, stop=False)
            k.mm(pr[:, :], ATP[:, 64:192], Hbd[1][:, :], start=False, stop=False)
            for h in range(2):
                hs = slice(h * 64, (h + 1) * 64)
                k.mm(pr[:, hs], SC[h][2][:, :], Vtm[:, hs], start=False, stop=(h == 1))
            evac(RUs[:, :], pr[:, :])
            pu = pn()
            for h in range(2):
                hs = slice(h * 64, (h + 1) * 64)
                k.mm(pu[:, hs], ainv[h][:, :], RUs[:, hs], start=True, stop=True)
            evac(Us[:, :], pu[:, :])
            py = pn()
            k.mm(py[:, :], RTP[:, 0:128], Hbd[0][:, :], start=True, stop=False)
            k.mm(py[:, :], RTP[:, 64:192], Hbd[1][:, :], start=False, stop=False)
            for h in range(2):
                hs = slice(h * 64, (h + 1) * 64)
                k.mm(py[:, hs], SC[h][3][:, :], Us[:, hs], start=False, stop=False)
                k.mm(py[:, hs], SC[h][4][:, :], Vtm[:, hs], start=False, stop=(h == 1))
            k.cp(Ys[:, :], py[:, :], e="act")
            ck(8)
            for q in range(2):
                qs = slice(q * 64, (q + 1) * 64)
                ph = pn()
                k.mm(ph[:, :], BHtm[qs, :], Us[qs, :], start=True, stop=False)
                k.mm(ph[:, :], KHtm[qs, :], Vtm[qs, :], start=False, stop=True)
                for h in range(2):
                    hs = slice(h * 64, (h + 1) * 64)
                    k.stt(Hbd[q][hs, hs], Hbd[q][hs, hs], GLc[hs, q:q + 1], ph[hs, hs], ALU.mult, ALU.add)
            ck(9)
            for h in range(2):
                hs = slice(h * 64, (h + 1) * 64)
                k.op("dve", lambda: nc.vector.reduce_sum(SUMc.t[:, h:h + 1], Ys.t[:, hs], axis=AX.X), [Ys[:, hs]], [SUMc[:, h:h + 1]])
            k.ts(MEAN[:, :], SUMc[:, :], 1.0 / 64, ALU.mult, e="pool")
            for h in range(2):
                hs = slice(h * 64, (h + 1) * 64)
                k.ts(YC[:, hs], Ys[:, hs], MEAN[:, h:h + 1], ALU.subtract, e="pool")
                k.act(YN[:, hs], YC[:, hs], AF.Square, accum=VSc[:, h:h + 1])
            rsqrt(RSG[:, :], VSc[:, :], 1.0 / 64, pc(P_EPSGN))
            for h in range(2):
                hs = slice(h * 64, (h + 1) * 64)
                k.ts(YN[:, hs], YC[:, hs], RSG[:, h:h + 1], ALU.mult, e="pool")
            k.tt(YN[:, :], YN[:, :], LNW[:, :], ALU.mult, e="pool")
            k.tt(YN[:, :], YN[:, :], LNB[:, :], ALU.add, e="pool")
            k.stt(RKR[:, :], XSr[:, :], pc(P_RK), KM[:, :], ALU.mult, ALU.mult)
            pb_ = pn()
            k.mm(pb_[:, 0:2], RKR[:, :], SEL2[:, 0:2])
            k.cp(SBs[:, :], pb_[:, 0:2])
            for h in range(2):
                hs = slice(h * 64, (h + 1) * 64)
                k.stt(YN[:, hs], Vtm[:, hs], SBs[:, h:h + 1], YN[:, hs], ALU.mult, ALU.add)
            k.tt(YOUT[:, 0:128], YN[:, :], SGR[:, :], ALU.mult)

            ck(10)
            for g in range(3):
                src = PG[4 + g]
                k.ts(v3(CV[g]), src[:, :, 0:64], pc(P_CW + g * 4), ALU.mult, e="pool")
                for i in range(1, 4):
                    k.stt(v3(CV[g]), src[:, :, i:i + 64], pc(P_CW + g * 4 + i), v3(CV[g]), ALU.mult, ALU.add)
                silu(AC[g][:, :], CV[g][:, :], STMP[:, :])
            ck(11)
            pq = pw()
            for g in range(3):
                k.tr(pq[:, g * 128:(g + 1) * 128], AC[g][:, :], IDENT[:, :])
            k.act(SQJ[:, 0:128], pq[:, 0:128], AF.Square, accum=SSQ[:, :])
            k.act(SQJ[:, 128:256], pq[:, 128:256], AF.Square, accum=SSK[:, :])
            rsqrt(RNQ[:, :], SSQ[:, :], 1.0, pc(P_EPSL2))
            rsqrt(RNK[:, :], SSK[:, :], 1.0, pc(P_EPSL2))
            k.ts(RNQ[:, :], RNQ[:, :], 128.0 ** -0.5, ALU.mult)
            sigm(BETA[:, :], BA[:, 0:1])
            k.act(E1[:, :], BA[:, 1:2], AF.Exp, bias=pc(P_DTB))
            k.act(SPc[:, :], E1[:, :], AF.Ln, bias=pc(P_ONE))
            k.tt(Gc[:, :], SPc[:, :], pc(P_NEGA), ALU.mult, e="pool")
            if masked:
                k.tt(BETA[:, :], BETA[:, :], SMCOL, ALU.mult)
                k.tt(Gc[:, :], Gc[:, :], SMCOL, ALU.mult)
            ck(12)
            pg_ = pn()
            k.mm(pg_[:, 0:1], MUI[:, :], Gc[:, :])
            k.mm(pg_[:, 1:2], BLK[:, :], Gc[:, :])
            k.cp(GCs[:, :], pg_[:, 0:2])
            k.act(EG[:, :], GCs[:, 0:1], AF.Exp)
            k.tt(DG[:, :], GCs[:, 1:2], GCs[:, 0:1], ALU.subtract, e="pool")
            k.act(EGL[:, :], DG[:, :], AF.Exp)
            prw = pn()
            k.mm(prw[0:1, :], Gc[:, :], MUI[:, :])
            k.cp(GROW[:, :], prw[0:1, :], e="act")
            k.ts(NGROW[:, :], prw[0:1, :], -1.0, ALU.mult)
            pd = pn(); pdt = pn()
            k.mm(pd[:, :], GROW[:, :], ONES[0:1, :], start=True, stop=False)
            k.mm(pd[:, :], ONES[0:1, :], NGROW[:, :], start=False, stop=True)
            k.mm(pdt[:, :], ONES[0:1, :], GROW[:, :], start=True, stop=False)
            k.mm(pdt[:, :], NGROW[:, :], ONES[0:1, :], start=False, stop=True)
            k.ts(DMc[:, :], pd[:, :], 0.0, ALU.min)
            k.act(DEC[:, :], DMc[:, :], AF.Exp)
            k.ts(DMTc[:, :], pdt[:, :], 0.0, ALU.min)
            k.act(DECT[:, :], DMTc[:, :], AF.Exp)
            k.ts(GBC[:, :], ONES[:, :], Gc[:, :], ALU.mult, e="pool")
            pe_ = pn()
            k.mm(pe_[:, 0:2], GBC[:, :], SEL2[:, 0:2])
            k.act(EGB[:, :], pe_[:, 0:2], AF.Exp)
            ck(13)
            k.ts(KN[:, :], pq[:, 128:256], RNK[:, :], ALU.mult)
            k.ts(QN[:, :], pq[:, 0:128], RNQ[:, :], ALU.mult)
            k.ts(VB[:, :], pq[:, 256:384], BETA[:, :], ALU.mult)
            k.ts(QG[:, :], QN[:, :], EG[:, :], ALU.mult, e="pool")
            k.ts(KB[:, :], KN[:, :], BETA[:, :], ALU.mult, e="pool")
            k.ts(KW[:, :], KB[:, :], EG[:, :], ALU.mult, e="pool")
            k.ts(KL[:, :], KN[:, :], EGL[:, :], ALU.mult, e="pool")
            p = pw()
            for i_, src in enumerate((KN, KB, QN)):
                k.tr(p[:, i_ * 128:(i_ + 1) * 128], src[:, :], IDENT[:, :])
            evac(KKQ[:, :], p[:, 0:384])
            p = pn()
            k.tr(p[:, :], QG[:, :], IDENT[:, :])
            evac(padv(QGP), p[:, :].re("p (s c) -> p s c", s=2))
            ck(14)
            P0g = sq_g0; PT0g = sq_g1
            pm = pn()
            k.mm(pm[:, :], KBT[:, :], KNT[:, :])
            k.tt(P0g[:, :], pm[:, :], DEC[:, :], ALU.mult)
            k.tt(P0g[:, :], P0g[:, :], NML[:, :], ALU.mult, e="pool")
            pm2 = pn()
            k.mm(pm2[:, :], KNT[:, :], KBT[:, :])
            k.tt(PT0g[:, :], pm2[:, :], DECT[:, :], ALU.mult)
            k.tt(PT0g[:, :], PT0g[:, :], NMU[:, :], ALU.mult, e="pool")
            pm3 = pn()
            k.mm(pm3[:, :], KNT[:, :], QNT[:, :])
            k.tt(PTs[:, :], pm3[:, :], DECT[:, :], ALU.mult)
            k.tt(PTs[:, :], PTs[:, :], MUI[:, :], ALU.mult, e="pool")
            AIg = neumann(2, P0g, PT0g, 3 if masked else 5)
            ck(15)
            pwt = pn()
            k.mm(pwt[:, :], KW[:, :], AIg[:, :])
            k.ts(padv(WTP), pwt[:, :].re("p (s c) -> p s c", s=2), -1.0, ALU.mult)
            pdl = pn()
            k.mm(pdl[:, :], AIg[:, :], VB[:, :], start=True, stop=False)
            k.mm(pdl[:, :], WTP[:, 0:128], Sg[0][:, :], start=False, stop=False)
            k.mm(pdl[:, :], WTP[:, 64:192], Sg[1][:, :], start=False, stop=True)
            evac(DEL[:, :], pdl[:, :])
            po = pn()
            k.mm(po[:, :], QGP[:, 0:128], Sg[0][:, :], start=True, stop=False)
            k.mm(po[:, :], QGP[:, 64:192], Sg[1][:, :], start=False, stop=False)
            k.mm(po[:, :], PTs[:, :], DEL[:, :], start=False, stop=True)
            k.cp(Os[:, :], po[:, :], e="act")
            for q in range(2):
                qs = slice(q * 64, (q + 1) * 64)
                psn = pn()
                k.mm(psn[:, :], KL[qs, :], DEL[qs, :])
                k.stt(Sg[q][:, :], Sg[q][:, :], EGB[:, q:q + 1], psn[:, :], ALU.mult, ALU.add)
            ck(16)
            k.act(SQJ[:, 256:384], Os[:, :], AF.Square, accum=SSO[:, :])
            rsqrt(RO[:, :], SSO[:, :], 1.0 / 128, pc(P_EPS))
            k.stt(TMPg[:, :], Os[:, :], RO[:, :], GNW[:, :], ALU.mult, ALU.mult)
            k.tt(YOUT[:, 128:256], TMPg[:, :], SGG[:, :], ALU.mult)
            k.dma("pool", View(YEXs[s // CH], YEXs[s // CH].t[(s % CH) * 32:(s % CH + 1) * 32, :].rearrange("a (s f) -> (a s) f", s=4)), YOUT[:, :])
            if debug:
                k.dma("pool", DBG[s * 128:(s + 1) * 128, :], YOUT[:, :])

            ck(17)
            if s == NPC or kind == "sample":
                oi = out_i[0]; out_i[0] += 1
                for g in range(4):
                    k.cp(OSHs[:, g * 2:(g + 1) * 2], PG[g][:, :, 3 + last], e="pool")
                for g in range(3):
                    k.cp(OCVs[:, g * 6:(g + 1) * 6].re("p (q h) -> p q h", q=2), PG[4 + g][:, :, last + 1:last + 4], e="pool")
                k.dma("pool", OSH[oi], OSHs[:, :])
                k.dma("pool", OCV[oi], OCVs[:, :])
                for q in range(2):
                    k.dma("pool", OWK[oi, q], Hbd[q][:, :])
                    k.dma("pool", OSS[oi, q], Sg[q][:, :])
            if kind != "sample" and s < NPC:
                for g in range(NG):
                    k.cp(PG[g][:, :, 0:3], PG[g][:, :, last + 1:last + 4], e="pool")

        sq_g0 = sq("sqg0"); sq_g1 = sq("sqg1")
        try:
            for s in range(NS if _DBG.get('nsteps') is None else _DBG['nsteps']):
                step(s)
        except _Cut:
            pass

        try:
          if phases >= 2:
              for kt in range(8):
                  for half in range(2):
                      st = STG[0]
                      k.dma("sp", st[:, 0:1024], WG[kt * 128:(kt + 1) * 128, half * 1024:(half + 1) * 1024])
                      k.ts(Wg[:, kt, half * 1024:(half + 1) * 1024], st[:, 0:1024], GN[:, kt:kt + 1], ALU.mult, e="pool")
              for wsrc, wdst in ((WOA, Woa), (WOB, Wob), (WOUT, Wout)):
                  for kt in range(8):
                      st = STG[0]
                      k.dma("sp", st[:, 0:1024], wsrc[kt * 128:(kt + 1) * 128, :])
                      k.cp(wdst[:, kt, :], st[:, 0:1024], e="pool")

              ck(198)
              for kc in range(NCH):
                  k.drain("pool", [YEXs[kc]])
                  ccs = es.enter_context(nc.semaphore("ccsem%d" % kc))
                  nc.gpsimd.collective_compute("AllGather", ALU.bypass, replica_groups=[list(range(8))],
                                               ins=[YEXs[kc].t[:, :]], outs=[YALLs[kc].t[:, :]]).then_inc(ccs)
                  YALLs[kc].sem = ccs; YALLs[kc].dma_val = 1; YALLs[kc].lw = ("dma", YALLs[kc])
                  k._wait("pool", YALLs[kc].lw)
              ck(199)
              YG = k.sb("yg", [128, 8, 256], BF16)
              YG2 = k.sb("yg2", [128, 2, 1024], BF16)
              YRT = k.sb("yrt", [128, 8, 128], BF16); YGT = k.sb("ygt", [128, 8, 128], BF16)
              GT = k.sb("gt", [128, 16, 128], BF16)
              MGa = k.sb("mga", [128, 128]); MGb = k.sb("mgb", [128, 128])
              MGT = k.sb("mgt", [128, 8, 128], BF16)
              SS3 = col("ss3", 2); R3 = col("r3")
              Y3 = XN
              for j in range(NT3):
                  X = XB[j % 2]
                  k.dma("sp", X[:, :], X3[j * 128:(j + 1) * 128, :])
                  for g2 in range(2):
                      kc3 = min((j * 8) // CH, NCH - 1)
                      k._deps("pool", [View(YALLs[kc3], None), IDXs[:, :]], [YG2[:, g2, :]])
                      if YG2.sem is None:
                          YG2.sem = es.enter_context(nc.semaphore("ds_yg2"))
                      ins = nc.gpsimd.indirect_dma_start(out=YG2.t[:, g2, :], out_offset=None, in_=YALLs[kc3].t[:, :],
                                                         in_offset=bass.IndirectOffsetOnAxis(ap=IDXs.t[:, j * 2 + g2:j * 2 + g2 + 1], axis=0),
                                                         bounds_check=8 * chn[kc3] * 32 - 1, oob_is_err=False)
                      YG2.dma_val += 16
                      ins.then_inc(YG2.sem, 16)
                      k._mark(("dma", YG2), [IDXs[:, :]], [YG2[:, g2, :]])
                  ck(201)
                  for g2 in range(2):
                      k.dma("sp", YSC[j][g2 * 128:(g2 + 1) * 128, :], YG2[:, g2, :])
                  ck(202)
                  for r in range(8):
                      k.dma("sp", YG[:, r, :], View(YSC[j], YSC[j].t[r * 32:(r + 1) * 32, :].rearrange("h (s f) -> (h s) f", s=4)))
                  ck(203)
                  norm_transpose(X)
                  ck(204)
                  XA = XB[(j + 1) % 2]
                  for half in range(2):
                      k.cp(XA[:, :].re("p (a b) -> p a b", a=4), YG[:, half * 4:half * 4 + 4, :])
                      for part, dst in ((0, YRT), (1, YGT)):
                          p = pw()
                          for jj in range(4):
                              k.tr(p[:, jj * 128:(jj + 1) * 128], XA[:, jj * 256 + part * 128:jj * 256 + part * 128 + 128], IDENT[:, :])
                          evac(dst[:, half * 4:half * 4 + 4, :], p[:, :].re("p (a b) -> p a b", a=4))
                  ck(205)
                  for m in range(16):
                      p = pn()
                      for kt in range(8):
                          k.mm(p[:, :], Wg[:, kt, m * 128:(m + 1) * 128], XNT[:, kt, :], start=(kt == 0), stop=(kt == 7))
                      k.act(STMP[:, :], p[:, :], AF.Exp, scale=-1.0)
                      recip1p(STMP[:, :])
                      k.cp(GT[:, m, :], STMP[:, :], e="pool")
                  ck(206)
                  for m in range(8):
                      p1 = pn(); p2 = pn()
                      for r in range(8):
                          k.mm(p1[:, :], Woa[:, r, m * 128:(m + 1) * 128], YRT[:, r, :], start=(r == 0), stop=(r == 7))
                      for r in range(8):
                          k.mm(p2[:, :], Wob[:, r, m * 128:(m + 1) * 128], YGT[:, r, :], start=(r == 0), stop=(r == 7))
                      k.tt(MGa[:, :], p1[:, :], GT[:, m, :], ALU.mult)
                      k.tt(MGb[:, :], p2[:, :], GT[:, 8 + m, :], ALU.mult)
                      k.tt(MGT[:, m, :], MGa[:, :], MGb[:, :], ALU.add, e="pool")
                  ck(207)
                  pos = []
                  for half in range(2):
                      p = pw()
                      for m in range(8):
                          k.mm(p[:, :], MGT[:, m, :], Wout[:, m, half * 512:(half + 1) * 512], start=(m == 0), stop=(m == 7))
                      k.act(SQJ[:, half * 512:(half + 1) * 512], p[:, :], AF.Square, accum=SS3[:, half:half + 1])
                      pos.append(p)
                  k.tt(SS3[:, 0:1], SS3[:, 0:1], SS3[:, 1:2], ALU.add)
                  rsqrt(R3[:, :], SS3[:, 0:1], 1.0 / D, pc(P_EPS))
                  for half in range(2):
                      hsl = slice(half * 512, (half + 1) * 512)
                      k.stt(Y3[:, hsl], pos[half][:, :], R3[:, :], NPB[:, hsl], ALU.mult, ALU.mult)
                  ck(208)
                  k.tt(Y3[:, :], Y3[:, :], X[:, :], ALU.add, e="pool")
                  k.dma("sp", OY[j * 128:(j + 1) * 128, :], Y3[:, :])
        except _Cut:
            k.drain('sp', [b_ for b_ in (locals().get('YG'), locals().get('YG2')) if b_ is not None])
        k.drain("pool", [XN, OSHs, OCVs, Hbd[0], Hbd[1], Sg[0], Sg[1], YOUT])
        k.drain("sp", [XB[0], XB[1]])
        print("instructions:", k.ninst)
    return nc


def _consts():
    i = np.arange(128)
    same = (i[:, None] // 64) == (i[None, :] // 64)
    ML = (same & (i[None, :] < i[:, None])).astype(np.float32)
    MUI = (same & (i[:, None] <= i[None, :])).astype(np.float32)
    cst = np.stack([np.eye(128, dtype=np.float32), ML, ML.T.copy(), MUI, same.astype(np.float32),
                    np.broadcast_to(((i % 64) < 16).astype(np.float32)[None, :], (128, 128)).copy(),
                    np.ones((128, 128), np.float32)])
    sel2 = np.stack([(i < 64), (i >= 64), (i % 64) < 16, np.ones(128, bool)], 1).astype(np.float32)
    return cst, sel2


def _prep(inp, NPC, NSP):
    f = lambda a: np.ascontiguousarray(np.asarray(a, dtype=np.float32))
    NS = 1 + NPC + NSP
    NSLOT = NS * 128
    NT3 = (NS + 7) // 8
    xp = f(inp["x_prompt"]); xsam = f(inp["x_sample"]); meta = f(inp["meta_tokens"])
    xs = np.zeros((NS, 128, D), np.float32)
    xs[0, 0:16] = meta; xs[0, 64:80] = meta
    for s in range(1, NPC + 1):
        xs[s, 0:64] = xp[0, (s - 1) * 64:s * 64]
        xs[s, 64:128] = xp[1, (s - 1) * 64:s * 64]
    for sp in range(NSP):
        xs[1 + NPC + sp, 0:16] = xsam[2 * sp]
        xs[1 + NPC + sp, 64:80] = xsam[2 * sp + 1]
    w_in = f(inp["w_in"])[0]
    cst, sel2 = _consts()
    mu = f(inp["rw_mu"])[0]; cw = f(inp["gd_conv_w"])[0]
    shst = f(inp["state_rwkv_shift"])[0]; cvst = f(inp["state_gdn_conv"])[0]
    wkst = f(inp["state_rwkv_wkv"])[0]; ssst = f(inp["state_gdn_ssm"])[0]
    gn = np.ascontiguousarray(f(inp["norm_pre"])[0].reshape(8, 128).T)
    npost = np.ascontiguousarray(np.broadcast_to(f(inp["norm_post"])[0][None, :], (128, D)))
    maps = []
    for c in range(8):
        cs = slice(128 * c, 128 * c + 128)
        cols = [w_in[:, 0 * 1024 + 128 * c:][:, :128], w_in[:, 1024 + 128 * c:][:, :128], w_in[:, 2048 + 128 * c:][:, :128],
                w_in[:, 3072:3200], w_in[:, 4224 + 128 * c:][:, :128], w_in[:, 5248 + 128 * c:][:, :128],
                w_in[:, 6272 + 128 * c:][:, :128], w_in[:, 3200 + 128 * c:][:, :128], w_in[:, 7312 + 128 * c:][:, :128],
                w_in[:, 7296 + c:7297 + c], w_in[:, 7304 + c:7305 + c]]
        wc = np.ascontiguousarray(np.concatenate(cols, 1))
        prm = np.zeros((128, 32), np.float32)
        prm[:, 0] = mu[0 + 128 * c:][:128]; prm[:, 1] = mu[1024 + 128 * c:][:128]; prm[:, 2] = mu[2048 + 128 * c:][:128]
        prm[:, 3] = mu[3072:3200]
        prm[:, 4] = f(inp["rw_w0"])[0][cs]; prm[:, 5] = f(inp["rw_a0"])[0][cs]
        prm[:, 6] = f(inp["rw_k_k"])[0][cs]; prm[:, 7] = f(inp["rw_k_a"])[0][cs]
        prm[:, 8] = f(inp["rw_r_k"])[0].reshape(-1)[cs]
        for g in range(3):
            for i in range(4):
                prm[:, 9 + g * 4 + i] = cw[i, g * 1024 + 128 * c:][:128]
        prm[:, 21] = f(inp["gd_dt_bias"])[0][c]; prm[:, 22] = f(inp["gd_a_log"])[0][c]
        prm[:, 23] = 1e-6; prm[:, 24] = 64e-5; prm[:, 25] = 1.0; prm[:, 26] = 1e-6
        bc = np.stack([np.broadcast_to(f(inp["rw_ln_w"])[0][cs][None, :], (128, 128)),
                       np.broadcast_to(f(inp["rw_ln_b"])[0][cs][None, :], (128, 128)),
                       np.broadcast_to(f(inp["gd_norm_w"])[0][None, :], (128, 128))]).astype(np.float32)
        w2a2 = np.concatenate([f(inp["rw_w2"])[0][:, cs], f(inp["rw_a2"])[0][:, cs]], 0)
        n1 = max(NSP, 1)
        sh0 = np.zeros((n1, 4, 128, 2), np.float32); cv0 = np.zeros((n1, 3, 128, 6), np.float32)
        hb0 = np.zeros((n1, 2, 128, 128), np.float32); ss0 = np.zeros((n1, 2, 128, 128), np.float32)
        for sp in range(NSP):
            for q in range(2):
                b = 2 * sp + q
                for g, ofs in enumerate((128 * c, 1024 + 128 * c, 2048 + 128 * c, 3072)):
                    sh0[sp, g, :, q] = shst[b, ofs:ofs + 128]
                for g in range(3):
                    cv0[sp, g, :, q * 3:q * 3 + 3] = cvst[b, :, g * 1024 + 128 * c:][:, :128].T
                for h in range(2):
                    hb0[sp, q, h * 64:(h + 1) * 64, h * 64:(h + 1) * 64] = wkst[b, 2 * c + h].T
                ss0[sp, q] = ssst[b, c]
        x3 = np.zeros((NT3, 128, D), np.float32)
        idx = np.zeros((128, NT3 * 2), np.int32)
        for j in range(NT3):
            t = j * 8 + c
            tt_ = t if t < NS else 0
            if t < NS:
                x3[j] = xs[t]
            for g2 in range(2):
                for rl in range(4):
                    CHh = _DBG.get('ch', 80)
                    kc = min((j * 8) // CHh, (NS + CHh - 1) // CHh - 1)
                    nk = min(CHh, NS - kc * CHh)
                    tl = tt_ - kc * CHh
                    if tl < 0 or tl >= nk:
                        tl = 0
                    idx[rl * 32:(rl + 1) * 32, j * 2 + g2] = ((4 * g2 + rl) * nk + tl) * 32 + np.arange(32)
        maps.append({
            "xs": xs.reshape(NSLOT, D), "x3": x3.reshape(NT3 * 128, D), "idx": idx, "wc": wc,
            "wg": np.ascontiguousarray(w_in[:, 8336:8336 + 2048]), "woa": f(inp["w_out_a"])[0], "wob": f(inp["w_out_b"])[0],
            "wout": f(inp["w_out"])[0], "w2a2": np.ascontiguousarray(w2a2), "prm": prm, "gn": gn, "bc": np.ascontiguousarray(bc),
            "npost": npost, "cst": cst, "sel2": sel2, "sh0": sh0, "cv0": cv0, "hb0": hb0, "ss0": ss0,
        })
    return maps


def _post(res, NPC, NSP, Bp, S):
    NS = 1 + NPC + NSP
    NT3 = (NS + 7) // 8
    DB = 2 * NSP
    yp = np.zeros((Bp, S, D), np.float32); ys = np.zeros((DB, 16, D), np.float32)
    psh = np.zeros((1, Bp, 3200), np.float32); pwk = np.zeros((1, Bp, 16, 64, 64), np.float32)
    pcv = np.zeros((1, Bp, 3, 3072), np.float32); pss = np.zeros((1, Bp, 8, 128, 128), np.float32)
    ssh = np.zeros((1, DB, 3200), np.float32); swk = np.zeros((1, DB, 16, 64, 64), np.float32)
    scv = np.zeros((1, DB, 3, 3072), np.float32); sss = np.zeros((1, DB, 8, 128, 128), np.float32)
    for c in range(8):
        r = res[c]
        oy = np.asarray(r["oy"]).reshape(NT3, 128, D)
        for j in range(NT3):
            t = j * 8 + c
            if t < 1 or t >= NS:
                continue
            if t <= NPC:
                for q in range(2):
                    yp[q, (t - 1) * 64:t * 64] = oy[j, q * 64:(q + 1) * 64]
            else:
                sp = t - 1 - NPC
                for q in range(2):
                    ys[2 * sp + q] = oy[j, q * 64:q * 64 + 16]
        osh = np.asarray(r["osh"]); ocv = np.asarray(r["ocv"]); owk = np.asarray(r["owk"]); oss = np.asarray(r["oss"])
        for oi in range(1 + NSP):
            for q in range(2):
                if oi == 0:
                    tsh, tcv, twk, tss, b = psh, pcv, pwk, pss, q
                else:
                    tsh, tcv, twk, tss, b = ssh, scv, swk, sss, 2 * (oi - 1) + q
                for g, ofs in enumerate((128 * c, 1024 + 128 * c, 2048 + 128 * c, 3072)):
                    if g == 3 and c != 0:
                        continue
                    tsh[0, b, ofs:ofs + 128] = osh[oi, :, g * 2 + q]
                for g in range(3):
                    tcv[0, b, :, g * 1024 + 128 * c:g * 1024 + 128 * c + 128] = ocv[oi, :, g * 6 + q * 3:g * 6 + q * 3 + 3].T
                for h in range(2):
                    twk[0, b, 2 * c + h] = owk[oi, q, h * 64:(h + 1) * 64, h * 64:(h + 1) * 64].T
                tss[0, b, c] = oss[oi, q]
    return (yp, ys, psh, pwk, pcv, pss, ssh, swk, scv, sss)


def kernel(_debug=False, _phases=3, **inputs):
    Bp, S, _ = inputs["x_prompt"].shape
    DB = inputs["x_sample"].shape[0]
    assert Bp == 2 and S % 64 == 0 and DB % 2 == 0 and inputs["x_sample"].shape[1] == 16
    NPC, NSP = S // 64, DB // 2
    nc = build(NPC, NSP, debug=_debug, phases=_phases)
    maps = _prep(inputs, NPC, NSP)
    res = run_bass_kernel_spmd(nc, maps, core_ids=list(range(8)))
    out = _post(res.results, NPC, NSP, Bp, S)
    if _debug:
        return out, [np.asarray(r["dbg"]) for r in res.results]
    return out
```
